# Optimizing a Trainium2 kernel written in Bass

```python
import math
import jax, jax.numpy as jnp
from jax import lax
import numpy as np

D_MODEL = 2048
BATCH = 2
SEQ = 8192
DEPTH = 1

HEAD_DIM = 64
N_Q_HEADS = 16
N_KV_HEADS = 2
Q_PER_KV = N_Q_HEADS // N_KV_HEADS
ATTN_WIDTH = N_Q_HEADS * HEAD_DIM
KV_WIDTH = N_KV_HEADS * HEAD_DIM
WINDOW = 128
BLOCK = 128
ROPE_THETA = 10000.0
CONV_WIDTH = D_MODEL - ATTN_WIDTH
CONV_GROUPS = CONV_WIDTH // HEAD_DIM
SHORT_CONV_K = 3
MIX_WIDTH = ATTN_WIDTH + CONV_WIDTH
IN_WIDTH = ATTN_WIDTH + 2 * KV_WIDTH + 3 * CONV_WIDTH
D_FF = 5632
FFN_CONV_K = 3
LN_EPS = 1e-5
DEEPNORM_ALPHA = (2 * DEPTH) ** 0.25
DEEPNORM_BETA = (8 * DEPTH) ** -0.25
NEG_INF = -1e30

kernel_name = "hymba_swa_sink_shortconv_convffn_deepnorm"


def layer_norm(x, g, b):
    xf = x.astype(jnp.float32)
    mu = jnp.mean(xf, axis=-1, keepdims=True)
    var = jnp.mean(jnp.square(xf - mu), axis=-1, keepdims=True)
    y = (xf - mu) * lax.rsqrt(var + LN_EPS) * g.astype(jnp.float32) + b.astype(jnp.float32)
    return y.astype(x.dtype)


def causal_depthwise_conv(x, w):
    k = w.shape[0]
    return lax.conv_general_dilated(
        x, w[:, None, :].astype(x.dtype), window_strides=(1,), padding=[(k - 1, 0)],
        dimension_numbers=("NWC", "WIO", "NWC"), feature_group_count=x.shape[-1])


def rope(x, positions):
    half = HEAD_DIM // 2
    inv_freq = ROPE_THETA ** (-jnp.arange(half, dtype=jnp.float32) / half)
    ang = positions.astype(jnp.float32)[:, None] * inv_freq[None, :]
    cos = jnp.cos(ang)[None, :, None, :]
    sin = jnp.sin(ang)[None, :, None, :]
    xf = x.astype(jnp.float32)
    x1, x2 = xf[..., :half], xf[..., half:]
    out = jnp.concatenate([x1 * cos - x2 * sin, x2 * cos + x1 * sin], axis=-1)
    return out.astype(x.dtype)


def sliding_window_gqa(q, k, v, sinks):
    b, s = q.shape[0], q.shape[1]
    n = s // BLOCK
    qb = q.reshape(b, n, BLOCK, N_KV_HEADS, Q_PER_KV, HEAD_DIM)

    def band(t):
        tb = t.reshape(b, n, BLOCK, N_KV_HEADS, HEAD_DIM)
        prev = jnp.pad(tb, ((0, 0), (1, 0), (0, 0), (0, 0), (0, 0)))[:, :-1]
        return jnp.concatenate([prev, tb], axis=2)

    kb, vb = band(k), band(v)
    scores = jnp.einsum("bnqkgd,bnskd->bnkgqs", qb.astype(jnp.float32),
                        kb.astype(jnp.float32)) * (HEAD_DIM ** -0.5)
    qi = jnp.arange(BLOCK)[:, None]
    kj = jnp.arange(2 * BLOCK)[None, :]
    diff = BLOCK + qi - kj
    kpos = (jnp.arange(n)[:, None, None] - 1) * BLOCK + kj[None]
    valid = (diff >= 0) & (diff < WINDOW) & (kpos >= 0)
    scores = jnp.where(valid[None, :, None, None], scores, NEG_INF)
    sink = jnp.broadcast_to(
        sinks.astype(jnp.float32).reshape(1, 1, N_KV_HEADS, Q_PER_KV, 1, 1),
        scores.shape[:-1] + (1,))
    probs = jax.nn.softmax(jnp.concatenate([scores, sink], axis=-1), axis=-1)[..., :-1]
    out = jnp.einsum("bnkgqs,bnskd->bnqkgd", probs.astype(v.dtype), vb)
    return out.reshape(b, s, ATTN_WIDTH)


def setup_inputs(seed: int = 0) -> dict:
    key = jax.random.key(seed)
    ks = jax.random.split(key, 12)
    f32 = jnp.float32
    x = jax.random.normal(ks[0], (BATCH, SEQ, D_MODEL), f32)
    w_in = jax.random.normal(ks[1], (DEPTH, D_MODEL, IN_WIDTH), f32) * D_MODEL ** -0.5
    attn_sinks = jax.random.normal(ks[2], (DEPTH, N_Q_HEADS), f32) * 0.5
    short_conv_w = jax.random.normal(ks[3], (DEPTH, SHORT_CONV_K, CONV_WIDTH), f32) * SHORT_CONV_K ** -0.5
    w_out = jax.random.normal(ks[4], (DEPTH, MIX_WIDTH, D_MODEL), f32) * (MIX_WIDTH ** -0.5 * DEEPNORM_BETA)
    ln1_g = 1.0 + 0.02 * jax.random.normal(ks[5], (DEPTH, D_MODEL), f32)
    ln1_b = 0.02 * jax.random.normal(ks[6], (DEPTH, D_MODEL), f32)
    ffn_w_up = jax.random.normal(ks[7], (DEPTH, D_MODEL, 2 * D_FF), f32) * D_MODEL ** -0.5
    ffn_conv_w = jax.random.normal(ks[8], (DEPTH, FFN_CONV_K, 2 * D_FF), f32) * FFN_CONV_K ** -0.5
    ffn_w_down = jax.random.normal(ks[9], (DEPTH, D_FF, D_MODEL), f32) * (D_FF ** -0.5 * DEEPNORM_BETA)
    ln2_g = 1.0 + 0.02 * jax.random.normal(ks[10], (DEPTH, D_MODEL), f32)
    ln2_b = 0.02 * jax.random.normal(ks[11], (DEPTH, D_MODEL), f32)
    return {"x": x, "w_in": w_in, "attn_sinks": attn_sinks, "short_conv_w": short_conv_w,
            "w_out": w_out, "ln1_g": ln1_g, "ln1_b": ln1_b, "ffn_w_up": ffn_w_up,
            "ffn_conv_w": ffn_conv_w, "ffn_w_down": ffn_w_down, "ln2_g": ln2_g, "ln2_b": ln2_b}


def reference(x, w_in, attn_sinks, short_conv_w, w_out, ln1_g, ln1_b,
              ffn_w_up, ffn_conv_w, ffn_w_down, ln2_g, ln2_b):
    b, s = x.shape[0], x.shape[1]
    positions = jnp.arange(s)
    split_pts = [ATTN_WIDTH,
                 ATTN_WIDTH + KV_WIDTH,
                 ATTN_WIDTH + 2 * KV_WIDTH,
                 ATTN_WIDTH + 2 * KV_WIDTH + CONV_WIDTH,
                 ATTN_WIDTH + 2 * KV_WIDTH + 2 * CONV_WIDTH]
    for l in range(DEPTH):
        proj = jnp.einsum("bsd,de->bse", x, w_in[l])
        q, k, v, gate_b, gate_c, h = jnp.split(proj, split_pts, axis=-1)
        q = rope(q.reshape(b, s, N_Q_HEADS, HEAD_DIM), positions)
        k = rope(k.reshape(b, s, N_KV_HEADS, HEAD_DIM), positions)
        v = v.reshape(b, s, N_KV_HEADS, HEAD_DIM)
        attn = sliding_window_gqa(q, k, v, attn_sinks[l])
        conv = gate_b * causal_depthwise_conv(gate_c * h, short_conv_w[l])
        mix = jnp.concatenate([attn, conv], axis=-1)
        y = jnp.einsum("bsm,md->bsd", mix, w_out[l])
        x = layer_norm(DEEPNORM_ALPHA * x + y, ln1_g[l], ln1_b[l])
        u = jnp.einsum("bsd,df->bsf", x, ffn_w_up[l])
        u = causal_depthwise_conv(u, ffn_conv_w[l])
        a, g = jnp.split(u, 2, axis=-1)
        y = jnp.einsum("bsf,fd->bsd", jax.nn.silu(a) * g, ffn_w_down[l])
        x = layer_norm(DEEPNORM_ALPHA * x + y, ln2_g[l], ln2_b[l])
    return x
```

```python
import math
import numpy as np
import concourse.bass as bass
import concourse.mybir as mybir
from concourse.bass_utils import run_bass_kernel_spmd

F32 = mybir.dt.float32
BF16 = mybir.dt.bfloat16
ALU = mybir.AluOpType
AF = mybir.ActivationFunctionType

ENGS = ("pe", "act", "dve", "pool", "sp")
NEG = -30000.0


class Cfg:
    def __init__(self, D=2048, NQ=16, F=5632, FQ=4, NB=8, NU=2, TILE=428, NW=256, BATCH=2, SEQ=8192,
                 NCORES=8, depth=1):
        self.D, self.NQ, self.F, self.FQ, self.NB, self.NU = D, NQ, F, FQ, NB, NU
        self.TILE, self.NW, self.BATCH, self.SEQ, self.NCORES = TILE, NW, BATCH, SEQ, NCORES
        self.KC = D // 128
        self.NQUAD = NQ // 4
        self.AW = NQ * 64
        self.ACH = self.AW // 128
        self.CW = D - self.AW
        self.CCH = self.CW // 128
        self.FCH = F // 128
        self.CQ = self.FCH // FQ
        self.T = NB * 128
        self.TX = self.T + 256
        self.TM = self.T + 128
        self.NJ = 5 + 2 * self.ACH + 3 * self.CCH
        self.ND = D // NW
        self.alpha = (2.0 * depth) ** 0.25
        assert self.FCH % FQ == 0 and self.ACH % 4 == 0 and D % NW == 0
        assert self.CQ * NW <= 2 * self.KC * 128


def tiles(a, b, step):
    out = []
    while a < b:
        n = min(step, b - a)
        out.append((a, n))
        a += n
    return out


class Prog:
    def __init__(self):
        self.q = {e: [] for e in ENGS}
        self.cnt = {e: 0 for e in ENGS}
        self.waited = {e: {} for e in ENGS}
        self.res = {}
        self.dcnt = {}
        self.semkeys = set("p_" + e for e in ENGS)
        self.sems = {}

    def _st(self, k):
        s = self.res.get(k)
        if s is None:
            s = self.res[k] = {"w": None, "r": {}}
        return s

    def _deps(self, eng, reads, writes):
        deps = {}

        def add(tok):
            if tok is not None and deps.get(tok[0], 0) < tok[1]:
                deps[tok[0]] = tok[1]

        for k in reads:
            add(self._st(k)["w"])
        for k in writes:
            s = self._st(k)
            add(s["w"])
            for sk, v in s["r"].items():
                add((sk, v))
        wt = self.waited[eng]
        for sk, v in deps.items():
            if wt.get(sk, 0) < v:
                wt[sk] = v
                self.q[eng].append(lambda e, sk=sk, v=v: e.wait_ge(self.sems[sk], v))

    def _mark(self, tok, reads, writes):
        for k in reads:
            s = self._st(k)
            if s["r"].get(tok[0], 0) < tok[1]:
                s["r"][tok[0]] = tok[1]
        for k in writes:
            s = self._st(k)
            s["w"] = tok
            s["r"] = {}

    def op(self, eng, fn, reads=(), writes=()):
        self._deps(eng, reads, writes)
        self.cnt[eng] += 1
        sk = "p_" + eng
        tok = (sk, self.cnt[eng])
        self.q[eng].append(lambda e, fn=fn, sk=sk: fn(e).then_inc(self.sems[sk], 1))
        self._mark(tok, reads, writes)
        return tok

    def dma(self, eng, semkey, out, in_, reads=(), writes=()):
        self._deps(eng, reads, writes)
        self.semkeys.add(semkey)
        self.dcnt[semkey] = self.dcnt.get(semkey, 0) + 1
        tok = (semkey, 16 * self.dcnt[semkey])
        self.q[eng].append(lambda e, out=out, in_=in_, semkey=semkey:
                           e.dma_start(out=out, in_=in_).then_inc(self.sems[semkey], 16))
        self._mark(tok, reads, writes)
        return tok

    def handoff(self, old_keys, new_keys):
        merged = {}
        for k in old_keys:
            s = self.res.get(k)
            if s is None:
                continue
            if s["w"] is not None and merged.get(s["w"][0], 0) < s["w"][1]:
                merged[s["w"][0]] = s["w"][1]
            for sk, v in s["r"].items():
                if merged.get(sk, 0) < v:
                    merged[sk] = v
        for k in new_keys:
            s = self._st(k)
            for sk, v in merged.items():
                if s["r"].get(sk, 0) < v:
                    s["r"][sk] = v

    def wait_all(self, eng, toks):
        wt = self.waited[eng]
        for sk, v in toks:
            if wt.get(sk, 0) < v:
                wt[sk] = v
                self.q[eng].append(lambda e, sk=sk, v=v: e.wait_ge(self.sems[sk], v))


def build_program(cfg):
    c = cfg
    D, KC, NQ, NQUAD, AW, ACH, CCH = c.D, c.KC, c.NQ, c.NQUAD, c.AW, c.ACH, c.CCH
    F, FCH, FQ, CQ, NB, NU, T, TX, TM = c.F, c.FCH, c.FQ, c.CQ, c.NB, c.NU, c.T, c.TX, c.TM
    TILE, NW, ND, NJ = c.TILE, c.NW, c.ND, c.NJ
    HS = KC * 128 * 2
    NHS = 8
    NPT = 6

    nc = bass.Bass("TRN2", target_bir_lowering=False)

    def din(name, shape):
        return nc.dram_tensor(name, list(shape), F32, kind="ExternalInput").ap()

    xT_d = din("xT", [NU, 128, KC, TX])
    xtm_d = din("xtm", [NU, TM, D])
    flag_d = din("flag", [NU, 128, 1])
    cos_d = din("cosT", [NU, 128, TX])
    sin_d = din("sinT", [NU, 128, TX])
    win_d = din("w_in", [NJ, 128, KC, 128])
    wout_d = din("w_out", [ND, 128, KC, NW])
    wup_d = din("w_up", [FCH, 2, 128, KC, 128])
    wdn_d = din("w_down", [FQ, ND, 128, CQ, NW])
    scw_d = din("scw", [128, CCH * 3])
    fcw_d = din("fcw", [128, FCH * 6])
    sink_d = din("sinks", [128, NQ])
    ln_d = din("ln", [4, 128, D])
    ident_d = din("ident", [128, 128])
    mask_d = din("masks", [2, 128, 512])
    out_d = nc.dram_tensor("out", [NU, T, D], F32, kind="ExternalOutput").ap()

    off = {}
    cur = 0

    def region(name, nbytes):
        nonlocal cur
        nbytes = (nbytes + 31) // 32 * 32
        off[name] = (cur, nbytes)
        cur += nbytes

    region("RX", KC * TX * 2)
    region("MIX", max(KC * TM * 2, 2 * D * 4 + D * 4 + D * 2, CQ * T * 2 + 2 * T * 4))
    region("RING", NHS * HS)
    region("CONST", 256 + 3 * 1024 + CCH * 12 + FCH * 24 + NQ * 4 + 2048)
    region("Z", (NB + 1) * D * 4)
    bc_need = (ACH * TM * 2 + 6 * TX * 2 + (NB + 2) * 2 * 66 * 2 + 32 + 2 * TX * 4 + NPT * 1024
               + 2 * AW * 2 + 4 * TILE * 4 + TILE * 4 + TX * 4 + TM * 4 + 128)
    g_need = 4 * ((2 + T) * 4 + 32)
    x_bytes = max(0, bc_need - (NB + 1) * D * 4, g_need)
    region("X", x_bytes)
    ARENA = cur
    assert ARENA <= 207 * 1024, ("SBUF over budget", ARENA)

    P = Prog()

    with nc.sbuf_tensor("arena", [128, ARENA // 4], F32) as arena, \
            nc.psum_tensor("ps", [128, 8, 512], F32) as ps:

        def view(base, dtype, shape):
            esz = 2 if dtype == BF16 else 4
            n = int(np.prod(shape)) * esz
            assert base % 4 == 0 and n % 4 == 0
            ap = arena[:, base // 4:(base + n) // 4]
            if dtype != F32:
                ap = ap.bitcast(dtype)
            if len(shape) == 2:
                ap = ap.rearrange("p (a b) -> p a b", a=shape[0])
            elif len(shape) == 3:
                ap = ap.rearrange("p (a b c) -> p a b c", a=shape[0], b=shape[1])
            return ap

        class Alloc:
            def __init__(self, base, limit):
                self.base, self.cur, self.limit = base, base, limit

            def get(self, dtype, shape):
                esz = 2 if dtype == BF16 else 4
                n = (int(np.prod(shape)) * esz + 31) // 32 * 32
                v = view(self.cur, dtype, shape)
                self.cur += n
                assert self.cur <= self.limit, ("region overflow", self.cur - self.limit)
                return v

        xT = view(off["RX"][0], BF16, [KC, TX])
        mixT = view(off["MIX"][0], BF16, [KC, TM])
        z = view(off["Z"][0], F32, [NB + 1, D])
        ca = Alloc(off["CONST"][0], off["CONST"][0] + off["CONST"][1])
        ident = ca.get(BF16, [128])
        maskD = ca.get(BF16, [512])
        maskP = ca.get(BF16, [512])
        maskF = ca.get(BF16, [512])
        scw = ca.get(F32, [CCH * 3])
        fcw = ca.get(F32, [FCH * 6])
        esink = ca.get(F32, [NQ])
        flag = ca.get(F32, [1])
        negb = ca.get(F32, [1])
        lnst2 = [ca.get(F32, [8, 6]) for _ in range(2)]
        lnmv2 = [ca.get(F32, [2]) for _ in range(2)]
        lnsd2 = [ca.get(F32, [1]) for _ in range(2)]
        lnrs2 = [ca.get(F32, [1]) for _ in range(2)]
        lnnm2 = [ca.get(F32, [1]) for _ in range(2)]
        den = ca.get(F32, [4])
        rden = ca.get(F32, [4])

        la = Alloc(off["MIX"][0], off["MIX"][0] + off["MIX"][1])
        ln_g = la.get(F32, [D])
        ln_b = la.get(F32, [D])
        ln_xb2 = [la.get(BF16, [D]) for _ in range(2)]
        fa = Alloc(off["MIX"][0], off["MIX"][0] + off["MIX"][1])
        hT = fa.get(BF16, [CQ, T])
        acc_a = fa.get(F32, [T])
        acc_g = fa.get(F32, [T])
        xa = Alloc(off["X"][0], off["X"][0] + off["X"][1])
        ubuf = [[xa.get(F32, [2 + T]) for _ in range(2)] for _ in range(2)]
        ba = Alloc(off["Z"][0], off["X"][0] + off["X"][1])
        qrot = ba.get(BF16, [ACH, TM])
        krot = ba.get(BF16, [2, TX])
        krotz = ba.get(BF16, [4, TX])
        vaug = ba.get(BF16, [NB + 2, 2, 66])
        cosT = ba.get(F32, [TX])
        sinT = ba.get(F32, [TX])
        pt = [ba.get(BF16, [512]) for _ in range(NPT)]
        attn_tm = [ba.get(BF16, [AW]) for _ in range(2)]
        rs = [ba.get(F32, [TILE]) for _ in range(4)]
        cs = ba.get(F32, [TILE])
        chbuf = ba.get(F32, [TX])
        cacc = ba.get(F32, [TM])

        def ringview(slot, shape):
            return view(off["RING"][0] + slot * HS, BF16, shape)

        K_BC = (["qrot%d" % i for i in range(ACH)] + ["krot%d" % i for i in range(2)] + ["krotz%d" % i for i in range(4)]
                + ["vaug", "tab", "cs", "chbuf", "cacc"]
                + ["pt%d" % i for i in range(NPT)] + ["attn0", "attn1"] + ["rs%d" % i for i in range(4)])
        K_Z = ["z%d" % i for i in range(NB + 1)]
        K_UB = ["ub%d%d" % (s, a) for s in range(2) for a in range(2)]
        K_MIX = ["mix%d" % i for i in range(KC)]
        K_LN = ["lnp", "lnxb0", "lnxb1"]
        K_FF = ["hT%d" % i for i in range(CQ)] + ["acca", "accg"]

        bank_ctr = [0]

        def nb():
            b = bank_ctr[0] % 8
            bank_ctr[0] += 1
            return b

        plan = []
        for u_ in range(NU):
            for j in range(5 + 2 * ACH + 3 * CCH):
                plan.append((("win", j), win_d[j], 1, [KC, 128]))
            for n_ in range(ND):
                plan.append((("wout", n_), wout_d[n_], 2, [KC, NW]))
            for q_ in range(FQ):
                for cq_ in range(CQ):
                    plan.append((("wup", q_ * CQ + cq_, 0), wup_d[q_ * CQ + cq_, 0], 1, [KC, 128]))
                    plan.append((("wup", q_ * CQ + cq_, 1), wup_d[q_ * CQ + cq_, 1], 1, [KC, 128]))
                for n_ in range(ND):
                    plan.append((("wdn", q_, n_), wdn_d[q_, n_], 2, [CQ, NW]))
        ring = {"ni": 0, "ng": 0, "cur": 0, "owner": [None] * NHS, "done": set(), "info": {}}

        def ring_try_issue():
            while ring["ni"] < len(plan):
                j = ring["ni"]
                _tag, src, nhs, shape = plan[j]
                cur_ = ring["cur"]
                if nhs == 2 and cur_ % 2 == 1:
                    cur_ += 1
                slot = cur_ % NHS
                owners = [ring["owner"][slot + i] for i in range(nhs)]
                if any(o is not None and o not in ring["done"] for o in owners):
                    break
                keys = ["ring%d" % (slot + i) for i in range(nhs)]
                v = ringview(slot, shape)
                P.dma("pool", "s_ring%d" % slot, v, src, reads=(), writes=keys)
                for i in range(nhs):
                    ring["owner"][slot + i] = j
                ring["info"][j] = (v, keys)
                ring["cur"] = cur_ + nhs
                ring["ni"] += 1

        def ring_get(tag):
            j = ring["ng"]
            if j >= ring["ni"]:
                ring_try_issue()
            assert j < ring["ni"], "ring stalled"
            assert plan[j][0] == tag, ("weight plan mismatch", plan[j][0], tag)
            ring["ng"] += 1
            v, keys = ring["info"][j]
            return v, keys, j

        def ring_release(*js):
            for j in js:
                ring["done"].add(j)
            ring_try_issue()

        def mm_group(outap, pairs, reads, writes):
            def fn(e, outap=outap, pairs=pairs):
                n = len(pairs)
                ins = None
                for i, (l, r) in enumerate(pairs):
                    ins = e.matmul(outap, l, r, start=(i == 0), stop=(i == n - 1))
                return ins
            return P.op("pe", fn, reads, writes)

        P.dma("sp", "s_c0", scw, scw_d, writes=["scw"])
        P.dma("sp", "s_c1", fcw, fcw_d, writes=["fcw"])
        P.dma("sp", "s_c2", esink, sink_d, writes=["esink"])
        P.dma("pool", "s_c3", ident, ident_d, writes=["ident"])
        P.dma("pool", "s_c4", maskD, mask_d[0], writes=["maskD"])
        P.dma("pool", "s_c5", maskP, mask_d[1], writes=["maskP"])
        P.op("act", lambda e: e.activation(out=esink, in_=esink, func=AF.Exp), reads=["esink"], writes=["esink"])

        out_toks = []

        ln_ctr = [0]

        def layer_norm(zrow, zkey, want_bf16):
            sl = ln_ctr[0] % 2
            ln_ctr[0] += 1
            lnst, lnmv, lnsd, lnrs, lnnm = lnst2[sl], lnmv2[sl], lnsd2[sl], lnrs2[sl], lnnm2[sl]
            kst, kmv, ksd, krs, knm = ("lnst%d" % sl, "lnmv%d" % sl, "lnsd%d" % sl, "lnrs%d" % sl, "lnnm%d" % sl)
            nchunk = (D + 511) // 512
            for j in range(nchunk):
                w = min(512, D - j * 512)
                P.op("dve", lambda e, j=j, w=w, lnst=lnst: e.bn_stats(out=lnst[:, j, :], in_=zrow[:, j * 512:j * 512 + w]),
                     reads=[zkey], writes=[kst])
            P.op("dve", lambda e, lnst=lnst, lnmv=lnmv: e.bn_aggr(out=lnmv, in_=lnst[:, 0:nchunk, :].rearrange("p a b -> p (a b)")),
                 reads=[kst], writes=[kmv])
            P.op("dve", lambda e, lnsd=lnsd, lnmv=lnmv: e.tensor_scalar(out=lnsd, in0=lnmv[:, 1:2], scalar1=1e-5, scalar2=None,
                                                                       op0=ALU.add), reads=[kmv], writes=[ksd])
            P.op("act", lambda e, lnsd=lnsd: e.activation(out=lnsd, in_=lnsd, func=AF.Sqrt), reads=[ksd], writes=[ksd])
            P.op("dve", lambda e, lnrs=lnrs, lnsd=lnsd: e.reciprocal(out=lnrs, in_=lnsd), reads=[ksd], writes=[krs])
            P.op("dve", lambda e, lnmv=lnmv: e.scalar_tensor_tensor(out=zrow, in0=zrow, scalar=lnmv[:, 0:1], in1=ln_g,
                                                                    op0=ALU.subtract, op1=ALU.mult),
                 reads=[zkey, kmv, "lnp"], writes=[zkey])
            P.op("dve", lambda e, lnrs=lnrs: e.scalar_tensor_tensor(out=zrow, in0=zrow, scalar=lnrs, in1=ln_b,
                                                                    op0=ALU.mult, op1=ALU.add),
                 reads=[zkey, krs, "lnp"], writes=[zkey])
            if want_bf16:
                xb = ln_xb2[sl]
                P.op("act", lambda e, xb=xb: e.activation(out=xb, in_=zrow, func=AF.Copy), reads=[zkey], writes=["lnxb%d" % sl])
            return sl

        def load_xT(u_):
            for k2 in range(0, KC, 2):
                P.dma("pool", "s_x%d" % k2, xT[:, k2:k2 + 2, :], xT_d[u_, :, k2:k2 + 2, :],
                      writes=["rx%d" % k2, "rx%d" % (k2 + 1)] + ["x1b%d" % bi for bi in range(NB + 1)])

        for u in range(NU):
            P.handoff(K_Z + K_UB, K_BC)
            P.handoff(K_LN + K_FF, K_MIX)
            P.dma("sp", "s_t0", cosT, cos_d[u], writes=["tab"])
            P.dma("sp", "s_t1", sinT, sin_d[u], writes=["tab"])
            P.dma("sp", "s_t2", flag, flag_d[u], writes=["flag"])
            P.op("dve", lambda e: e.tensor_scalar(out=negb, in0=flag, scalar1=-1.0, scalar2=-NEG,
                                                  op0=ALU.add, op1=ALU.mult), reads=["flag"], writes=["negb"])
            P.op("dve", lambda e: e.tensor_scalar(out=maskF, in0=maskP, scalar1=negb, scalar2=None, op0=ALU.add),
                 reads=["negb", "maskP"], writes=["maskF"])
            P.op("dve", lambda e: e.memset(vaug[:, :, :, 64:66], 1.0), writes=["vaug"])
            if u == 0:
                load_xT(0)
            RX = ["rx%d" % k for k in range(KC)]

            def proj_group(wv, wkeys, t0, n, b):
                mm_group(ps[:, b, 0:n], [(wv[:, kc, :], xT[:, kc, t0:t0 + n]) for kc in range(KC)],
                         reads=wkeys + RX, writes=["ps%d" % b])

            rope_ctr = [0]

            def rope_chunk(j0, dst, dkey, lo):
                wa, ka, ja_ = ring_get(("win", j0))
                wb, kb, jb_ = ring_get(("win", j0 + 1))
                for (t0, n) in tiles(lo, TX, TILE):
                    b1, b2 = nb(), nb()
                    proj_group(wa, ka, t0, n, b1)
                    proj_group(wb, kb, t0, n, b2)
                    si = 2 * (rope_ctr[0] % 2)
                    rope_ctr[0] += 1
                    r0, r1 = rs[si], rs[si + 1]
                    cc, ss = cosT[:, t0:t0 + n], sinT[:, t0:t0 + n]
                    P.op("dve", lambda e, b1=b1, cc=cc, n=n, r0=r0: e.tensor_tensor(out=r0[:, 0:n], in0=ps[:, b1, 0:n], in1=cc, op=ALU.mult),
                         reads=["ps%d" % b1, "tab"], writes=["rs%d" % si])
                    P.op("dve", lambda e, b2=b2, ss=ss, n=n, r1=r1: e.tensor_tensor(out=r1[:, 0:n], in0=ps[:, b2, 0:n], in1=ss, op=ALU.mult),
                         reads=["ps%d" % b2, "tab"], writes=["rs%d" % (si + 1)])
                    dd = dst[:, t0 - lo:t0 - lo + n]
                    P.op("pool", lambda e, dd=dd, n=n, r0=r0, r1=r1: e.tensor_tensor(out=dd, in0=r0[:, 0:n], in1=r1[:, 0:n], op=ALU.add),
                         reads=["rs%d" % si, "rs%d" % (si + 1)], writes=[dkey])
                ring_release(ja_, jb_)

            for kv in range(2):
                rope_chunk(2 * kv, krot[:, kv, :], "krot%d" % kv, 0)
                for half in range(2):
                    zi = 2 * kv + half
                    p0, z0 = 64 * half, 64 * (1 - half)
                    P.op("dve", lambda e, zi=zi, p0=p0, kv=kv: e.tensor_copy(out=krotz[p0:p0 + 64, zi, :], in_=krot[p0:p0 + 64, kv, :]),
                         reads=["krot%d" % kv], writes=["krotz%d" % zi])
                    P.op("dve", lambda e, zi=zi, z0=z0: e.memset(krotz[z0:z0 + 64, zi, :], 0.0), writes=["krotz%d" % zi])
            wv, kvk, jv_ = ring_get(("win", 4))
            for kb0 in range(0, NB + 2, 4):
                nk = min(4, NB + 2 - kb0)
                b = nb()

                def fnv(e, kb0=kb0, nk=nk, b=b, wv=wv):
                    ins = None
                    for j in range(nk):
                        kb = kb0 + j
                        for kc in range(KC):
                            ins = e.matmul(ps[:, b, j * 128:(j + 1) * 128], xT[:, kc, kb * 128:(kb + 1) * 128],
                                           wv[:, kc, :], start=(kc == 0), stop=(kc == KC - 1))
                    return ins
                P.op("pe", fnv, reads=kvk + RX, writes=["ps%d" % b])
                P.op("act", lambda e, kb0=kb0, nk=nk, b=b: e.activation(
                    out=vaug[:, kb0:kb0 + nk, :, 0:64],
                    in_=ps[:, b, 0:nk * 128].rearrange("p (a b c) -> p a b c", a=nk, b=2), func=AF.Copy),
                    reads=["ps%d" % b], writes=["vaug"])
            ring_release(jv_)
            for j in range(ACH):
                rope_chunk(5 + 2 * j, qrot[:, j, :], "qrot%d" % j, 128)
            for cc_ in range(CCH):
                j0 = 5 + 2 * ACH + 3 * cc_
                wC, kC, jC_ = ring_get(("win", j0))
                wH, kH, jH_ = ring_get(("win", j0 + 1))
                wB, kB_, jB_ = ring_get(("win", j0 + 2))
                for (t0, n) in tiles(126, TX, TILE):
                    b1, b2 = nb(), nb()
                    proj_group(wC, kC, t0, n, b1)
                    proj_group(wH, kH, t0, n, b2)
                    P.op("act", lambda e, b1=b1, n=n: e.activation(out=cs[:, 0:n], in_=ps[:, b1, 0:n], func=AF.Copy),
                         reads=["ps%d" % b1], writes=["cs"])
                    P.op("dve", lambda e, b2=b2, n=n, t0=t0: e.tensor_tensor(out=chbuf[:, t0:t0 + n], in0=ps[:, b2, 0:n],
                                                                      in1=cs[:, 0:n], op=ALU.mult),
                         reads=["ps%d" % b2, "cs"], writes=["chbuf"])
                ring_release(jC_, jH_)
                w0, w1, w2 = (scw[:, 3 * cc_ + k:3 * cc_ + k + 1] for k in range(3))
                P.op("dve", lambda e, w2=w2: e.tensor_scalar(out=cacc, in0=chbuf[:, 128:TX], scalar1=w2, scalar2=None, op0=ALU.mult),
                     reads=["chbuf", "scw"], writes=["cacc"])
                P.op("dve", lambda e, w1=w1: e.scalar_tensor_tensor(out=cacc, in0=chbuf[:, 127:TX - 1], scalar=w1, in1=cacc,
                                                                    op0=ALU.mult, op1=ALU.add),
                     reads=["chbuf", "scw", "cacc"], writes=["cacc"])
                P.op("dve", lambda e, w0=w0: e.scalar_tensor_tensor(out=cacc, in0=chbuf[:, 126:TX - 2], scalar=w0, in1=cacc,
                                                                    op0=ALU.mult, op1=ALU.add),
                     reads=["chbuf", "scw", "cacc"], writes=["cacc"])
                for (t0, n) in tiles(128, TX, TILE):
                    b = nb()
                    proj_group(wB, kB_, t0, n, b)
                    P.op("dve", lambda e, b=b, n=n, t0=t0, cc_=cc_: e.tensor_tensor(
                        out=mixT[:, ACH + cc_, t0 - 128:t0 - 128 + n], in0=ps[:, b, 0:n], in1=cacc[:, t0 - 128:t0 - 128 + n],
                        op=ALU.mult), reads=["ps%d" % b, "cacc"], writes=["mix%d" % (ACH + cc_)])
                ring_release(jB_)

            pt_ctr = [0]

            def emit_scores(i, m):
                qi = i - 1
                kv = (2 * m) // (ACH // 2)
                pts = []
                for half in range(2):
                    b = nb()
                    mkF = maskF if i == 2 else maskP
                    mkFk = "maskF" if i == 2 else "maskP"

                    def fns(e, b=b, mkF=mkF, kv=kv, m=m, qi=qi, half=half, i=i):
                        e.matmul(ps[:, b, 0:256], ident, mkF[:, 0:256], start=True, stop=False)
                        e.matmul(ps[:, b, 256:512], ident, maskD[:, 0:256], start=False, stop=False)
                        ins = None
                        p0 = 64 * half
                        for sel in range(2):
                            kb = i - 1 + sel
                            for cj in range(2):
                                col = (2 * sel + cj) * 128
                                ins = e.matmul(ps[:, b, col:col + 128],
                                               krotz[:, 2 * kv + half, kb * 128:(kb + 1) * 128],
                                               qrot[:, 2 * m + cj, qi * 128:(qi + 1) * 128],
                                               start=False, stop=(sel == 1 and cj == 1))
                        return ins
                    P.op("pe", fns, reads=["ident", mkFk, "maskD", "krotz%d" % (2 * kv + half), "qrot%d" % (2 * m), "qrot%d" % (2 * m + 1)],
                         writes=["ps%d" % b])
                    sl = pt_ctr[0] % NPT
                    pt_ctr[0] += 1
                    P.op("act", lambda e, b=b, sl=sl: e.activation(out=pt[sl], in_=ps[:, b, 0:512], func=AF.Exp, scale=0.125),
                         reads=["ps%d" % b], writes=["pt%d" % sl])
                    pts.append(sl)
                return pts

            def emit_pv(i, m, pts):
                qi = i - 1
                kv = (2 * m) // (ACH // 2)
                at = attn_tm[i % 2]
                atk = "attn%d" % (i % 2)
                bO = nb()

                def fno(e, bO=bO, pts=pts, kv=kv, i=i):
                    ins = None
                    for g in range(4):
                        pth = pt[pts[g % 2]]
                        cj = g // 2
                        e.matmul(ps[:, bO, g * 66:g * 66 + 66], pth[:, cj * 128:(cj + 1) * 128],
                                 vaug[:, i - 1, kv, 0:66], start=True, stop=False)
                        ins = e.matmul(ps[:, bO, g * 66:g * 66 + 66], pth[:, (2 + cj) * 128:(3 + cj) * 128],
                                       vaug[:, i, kv, 0:66], start=False, stop=True)
                    return ins
                P.op("pe", fno, reads=["pt%d" % pts[0], "pt%d" % pts[1], "vaug"], writes=["ps%d" % bO])
                pv = ps[:, bO, 0:264].rearrange("p (g e) -> p g e", e=66)
                P.op("dve", lambda e, pv=pv, m=m: e.tensor_tensor(out=den, in0=pv[:, :, 64], in1=esink[:, 4 * m:4 * m + 4], op=ALU.add),
                     reads=["ps%d" % bO, "esink"], writes=["den"])
                P.op("dve", lambda e: e.reciprocal(out=rden, in_=den), reads=["den"], writes=["rden"])
                for g in range(4):
                    h = 4 * m + g
                    P.op("dve", lambda e, pv=pv, g=g, h=h, at=at: e.tensor_scalar(
                        out=at[:, h * 64:(h + 1) * 64], in0=pv[:, g, 0:64], scalar1=rden[:, g:g + 1], scalar2=None,
                        op0=ALU.mult), reads=["ps%d" % bO, "rden"], writes=[atk])
                if m == NQUAD - 1:
                    for a0 in range(0, ACH, 8):
                        na = min(8, ACH - a0)
                        bT = nb()
                        pT = ps[:, bT, :].bitcast(BF16)

                        def fnt(e, a0=a0, na=na, pT=pT, at=at):
                            ins = None
                            for j in range(na):
                                ins = e.transpose(pT[:, j * 128:(j + 1) * 128], at[:, (a0 + j) * 128:(a0 + j + 1) * 128], ident)
                            return ins
                        P.op("pe", fnt, reads=[atk, "ident"], writes=["ps%d" % bT])
                        P.op("act", lambda e, a0=a0, na=na, pT=pT, qi=qi: e.activation(
                            out=mixT[:, a0:a0 + na, qi * 128:(qi + 1) * 128],
                            in_=pT[:, 0:na * 128].rearrange("p (a b) -> p a b", a=na), func=AF.Copy),
                            reads=["ps%d" % bT], writes=["mix%d" % j for j in range(a0, a0 + na)])

            prev = None
            for i in range(1, NB + 2):
                for m in range(NQUAD):
                    pts_ = emit_scores(i, m)
                    if prev is not None:
                        emit_pv(*prev)
                    prev = (i, m, pts_)
            emit_pv(*prev)

            P.handoff(K_BC, K_Z + K_UB)
            for i in range(NB + 1):
                P.dma("sp", "s_z%d" % i, z[:, i, :], xtm_d[u, i * 128:(i + 1) * 128, :], writes=["z%d" % i])
            for n_ in range(ND):
                wo, wok, jo_ = ring_get(("wout", n_))
                for i in range(NB + 1):
                    b = nb()
                    mm_group(ps[:, b, 0:NW], [(mixT[:, kc, i * 128:(i + 1) * 128], wo[:, kc, :]) for kc in range(KC)],
                             reads=wok + K_MIX, writes=["ps%d" % b])
                    zs = z[:, i, n_ * NW:(n_ + 1) * NW]
                    P.op("dve", lambda e, zs=zs, b=b: e.scalar_tensor_tensor(out=zs, in0=zs, scalar=c.alpha, in1=ps[:, b, 0:NW],
                                                                        op0=ALU.mult, op1=ALU.add),
                         reads=["ps%d" % b, "z%d" % i], writes=["z%d" % i])
                ring_release(jo_)

            P.handoff(K_MIX, K_LN)
            P.dma("sp", "s_l0", ln_g, ln_d[0], writes=["lnp"])
            P.dma("sp", "s_l1", ln_b, ln_d[1], writes=["lnp"])
            for i in range(NB + 1):
                lsl = layer_norm(z[:, i, :], "z%d" % i, True)
                ln_xb = ln_xb2[lsl]
                lxk = "lnxb%d" % lsl
                for k0 in range(0, KC, 8):
                    nk = min(8, KC - k0)
                    bT = nb()
                    pT = ps[:, bT, :].bitcast(BF16)

                    def fnt2(e, k0=k0, nk=nk, pT=pT, ln_xb=ln_xb):
                        ins = None
                        for j in range(nk):
                            ins = e.transpose(pT[:, j * 128:(j + 1) * 128], ln_xb[:, (k0 + j) * 128:(k0 + j + 1) * 128], ident)
                        return ins
                    P.op("pe", fnt2, reads=[lxk, "ident"], writes=["ps%d" % bT])
                    src = pT[:, 0:nk * 128].rearrange("p (a b) -> p a b", a=nk)
                    wk = ["rx%d" % k for k in range(k0, k0 + nk)] + ["x1b%d" % i]
                    if i == 0:
                        P.op("dve", lambda e, k0=k0, nk=nk, src=src: e.tensor_scalar(
                            out=xT[:, k0:k0 + nk, 0:2], in0=src[:, :, 126:128], scalar1=flag, scalar2=None, op0=ALU.mult),
                            reads=["ps%d" % bT, "flag"], writes=wk)
                    else:
                        P.op("act", lambda e, k0=k0, nk=nk, src=src, i=i: e.activation(
                            out=xT[:, k0:k0 + nk, 2 + (i - 1) * 128:2 + i * 128], in_=src, func=AF.Copy),
                            reads=["ps%d" % bT], writes=wk)

            P.handoff(K_LN, K_FF)
            ftiles = tiles(0, 2 + T, (2 + T + 2) // 3 if 2 + T > TILE else TILE)
            ftiles = tiles(0, 2 + T, min(TILE, -(-(2 + T) // (-(-(2 + T) // TILE)))))
            for q in range(FQ):
                for cq in range(CQ):
                    ch = q * CQ + cq
                    wa, wak, jwa_ = ring_get(("wup", ch, 0))
                    wg, wgk, jwg_ = ring_get(("wup", ch, 1))
                    us = ch % 2
                    ua, ug = ubuf[us]
                    for (t0, n) in ftiles:
                        b1, b2 = nb(), nb()
                        xk = ["x1b%d" % bi for bi in range(NB + 1)
                              if (max(0, 2 + (bi - 1) * 128) if bi > 0 else 0) < t0 + n and (2 + bi * 128 if bi > 0 else 2) > t0]
                        mm_group(ps[:, b1, 0:n], [(wa[:, kc, :], xT[:, kc, t0:t0 + n]) for kc in range(KC)],
                                 reads=wak + xk, writes=["ps%d" % b1])
                        mm_group(ps[:, b2, 0:n], [(wg[:, kc, :], xT[:, kc, t0:t0 + n]) for kc in range(KC)],
                                 reads=wgk + xk, writes=["ps%d" % b2])
                        P.op("act", lambda e, b1=b1, t0=t0, n=n, ua=ua: e.activation(out=ua[:, t0:t0 + n], in_=ps[:, b1, 0:n], func=AF.Copy),
                             reads=["ps%d" % b1], writes=["ub%d0" % us])
                        P.op("act", lambda e, b2=b2, t0=t0, n=n, ug=ug: e.activation(out=ug[:, t0:t0 + n], in_=ps[:, b2, 0:n], func=AF.Copy),
                             reads=["ps%d" % b2], writes=["ub%d1" % us])
                    ring_release(jwa_, jwg_)
                    for (ub_, ac, ag_, uk, ak) in ((ua, acc_a, 0, "ub%d0" % us, "acca"), (ug, acc_g, 1, "ub%d1" % us, "accg")):
                        wb_ = 6 * ch + 3 * ag_
                        w0, w1, w2 = (fcw[:, wb_ + k:wb_ + k + 1] for k in range(3))
                        P.op("dve", lambda e, ub_=ub_, ac=ac, w2=w2: e.tensor_scalar(out=ac, in0=ub_[:, 2:2 + T], scalar1=w2, scalar2=None, op0=ALU.mult),
                             reads=[uk, "fcw"], writes=[ak])
                        P.op("dve", lambda e, ub_=ub_, ac=ac, w1=w1: e.scalar_tensor_tensor(out=ac, in0=ub_[:, 1:1 + T], scalar=w1, in1=ac,
                                                                                   op0=ALU.mult, op1=ALU.add),
                             reads=[uk, "fcw", ak], writes=[ak])
                        P.op("dve", lambda e, ub_=ub_, ac=ac, w0=w0: e.scalar_tensor_tensor(out=ac, in0=ub_[:, 0:T], scalar=w0, in1=ac,
                                                                                   op0=ALU.mult, op1=ALU.add),
                             reads=[uk, "fcw", ak], writes=[ak])
                    P.op("act", lambda e: e.activation(out=acc_a, in_=acc_a, func=AF.Silu), reads=["acca"], writes=["acca"])
                    P.op("dve", lambda e, cq=cq: e.tensor_tensor(out=hT[:, cq, :], in0=acc_a, in1=acc_g, op=ALU.mult),
                         reads=["acca", "accg"], writes=["hT%d" % cq])
                for n_ in range(ND):
                    wd, wdk, jd_ = ring_get(("wdn", q, n_))
                    for i in range(NB):
                        b = nb()
                        mm_group(ps[:, b, 0:NW], [(hT[:, cq, i * 128:(i + 1) * 128], wd[:, cq, :]) for cq in range(CQ)],
                                 reads=wdk + ["hT%d" % cq for cq in range(CQ)], writes=["ps%d" % b])
                        zs = z[:, i + 1, n_ * NW:(n_ + 1) * NW]
                        zk = "z%d" % (i + 1)
                        if q == 0:
                            P.op("dve", lambda e, zs=zs, b=b: e.scalar_tensor_tensor(out=zs, in0=zs, scalar=c.alpha, in1=ps[:, b, 0:NW],
                                                                                op0=ALU.mult, op1=ALU.add),
                                 reads=["ps%d" % b, zk], writes=[zk])
                        else:
                            P.op("dve", lambda e, zs=zs, b=b: e.tensor_tensor(out=zs, in0=zs, in1=ps[:, b, 0:NW], op=ALU.add),
                                 reads=["ps%d" % b, zk], writes=[zk])
                    ring_release(jd_)

            if u + 1 < NU:
                load_xT(u + 1)
            P.handoff(K_FF, K_LN)
            P.dma("sp", "s_l0", ln_g, ln_d[2], writes=["lnp"])
            P.dma("sp", "s_l1", ln_b, ln_d[3], writes=["lnp"])
            for i in range(NB):
                layer_norm(z[:, i + 1, :], "z%d" % (i + 1), False)
                out_toks.append(P.dma("sp", "s_o%d" % i, out_d[u, i * 128:(i + 1) * 128, :], z[:, i + 1, :],
                                      reads=["z%d" % (i + 1)]))

        P.wait_all("sp", out_toks)

        import contextlib
        with contextlib.ExitStack() as es:
            for sk in sorted(P.semkeys):
                P.sems[sk] = es.enter_context(nc.semaphore(sk))
            with nc.Block() as block:
                @block.tensor
                def _(eng):
                    for f in P.q["pe"]:
                        f(eng)

                @block.scalar
                def _(eng):
                    for f in P.q["act"]:
                        f(eng)

                @block.vector
                def _(eng):
                    for f in P.q["dve"]:
                        f(eng)

                @block.gpsimd
                def _(eng):
                    for f in P.q["pool"]:
                        f(eng)

                @block.sync
                def _(eng):
                    for f in P.q["sp"]:
                        f(eng)
    return nc


def _win_cols(cfg):
    c = cfg
    AW, CW = c.AW, c.CW
    i = np.arange(128)
    sw = (i // 64) * 64 + ((i % 64) + 32) % 64
    cols = []
    for kv in range(2):
        cols.append(AW + kv * 64 + (i % 64))
        cols.append(AW + kv * 64 + (sw % 64))
    cols.append(AW + 128 + i)
    for j in range(c.ACH):
        cols.append(j * 128 + i)
        cols.append(j * 128 + sw)
    base = AW + 256
    for cc in range(c.CCH):
        cols.append(base + CW + cc * 128 + i)
        cols.append(base + 2 * CW + cc * 128 + i)
        cols.append(base + cc * 128 + i)
    return cols


def make_in_maps(cfg, x, w_in, attn_sinks, short_conv_w, w_out, ln1_g, ln1_b, ffn_w_up, ffn_conv_w, ffn_w_down,
                 ln2_g, ln2_b):
    c = cfg
    D, KC, T, TX, TM, NU = c.D, c.KC, c.T, c.TX, c.TM, c.NU
    f32 = np.float32
    x = np.asarray(x, f32)
    w_in = np.asarray(w_in, f32)[0]
    w_out = np.asarray(w_out, f32)[0]
    w_up = np.asarray(ffn_w_up, f32)[0]
    w_dn = np.asarray(ffn_w_down, f32)[0]
    cols = _win_cols(c)
    win_l = np.stack([w_in[:, cj].reshape(KC, 128, 128).transpose(1, 0, 2) for cj in cols]).astype(f32)
    wout_l = np.ascontiguousarray(w_out.reshape(KC, 128, c.ND, c.NW).transpose(2, 1, 0, 3))
    wup_l = np.ascontiguousarray(w_up.reshape(KC, 128, 2, c.FCH, 128).transpose(3, 2, 1, 0, 4))
    wdn_l = np.ascontiguousarray(w_dn.reshape(c.FQ, c.CQ, 128, c.ND, c.NW).transpose(0, 3, 2, 1, 4))
    scw = np.ascontiguousarray(np.asarray(short_conv_w, f32)[0].reshape(3, c.CCH, 128).transpose(2, 1, 0)).reshape(128, c.CCH * 3)
    fcw = np.ascontiguousarray(np.asarray(ffn_conv_w, f32)[0].reshape(3, 2, c.FCH, 128).transpose(3, 2, 1, 0)).reshape(128, c.FCH * 6)
    sinks = np.ascontiguousarray(np.broadcast_to(np.asarray(attn_sinks, f32)[0][None, :], (128, c.NQ)))
    ln = np.stack([np.broadcast_to(np.asarray(a, f32)[0][None, :], (128, D)) for a in (ln1_g, ln1_b, ln2_g, ln2_b)]).astype(f32)
    ident = np.eye(128, dtype=f32)
    kk = np.arange(128)[:, None]
    qq = np.arange(128)[None, :]
    mD = np.where(qq >= kk, 0.0, NEG).astype(f32)
    mP = np.where(kk > qq, 0.0, NEG).astype(f32)
    masks = np.stack([np.tile(mD, (1, 4)), np.tile(mP, (1, 4))]).astype(f32)
    inv_freq = (np.float32(10000.0) ** (-np.arange(32, dtype=f32) / np.float32(32))).astype(f32)
    shared = {"w_in": win_l, "w_out": wout_l, "w_up": wup_l, "w_down": wdn_l, "scw": scw, "fcw": fcw,
              "sinks": sinks, "ln": ln, "ident": ident, "masks": masks}
    cps = c.NCORES // c.BATCH
    per_core = c.SEQ // cps
    assert per_core == NU * T
    in_maps = []
    for core in range(c.NCORES):
        b, r = core // cps, core % cps
        xT = np.zeros((NU, 128, KC, TX), f32)
        xtm = np.zeros((NU, TM, D), f32)
        flag = np.zeros((NU, 128, 1), f32)
        cosT = np.zeros((NU, 128, TX), f32)
        sinT = np.zeros((NU, 128, TX), f32)
        for u in range(NU):
            s0 = r * per_core + u * T
            lo = s0 - 256
            seg = np.zeros((TX, D), f32)
            a = max(lo, 0)
            seg[a - lo:] = x[b, a:s0 + T]
            xT[u] = seg.T.reshape(KC, 128, TX).transpose(1, 0, 2)
            xtm[u] = seg[128:]
            flag[u] = 1.0 if s0 > 0 else 0.0
            pos = (np.arange(lo, s0 + T)).astype(f32)
            ang = pos[None, :] * inv_freq[:, None]
            cs_, sn_ = np.cos(ang).astype(f32), np.sin(ang).astype(f32)
            cosT[u] = np.tile(cs_, (4, 1))
            sinT[u] = np.concatenate([-sn_, sn_, -sn_, sn_], axis=0)
        m = dict(shared)
        m.update({"xT": xT, "xtm": xtm, "flag": flag, "cosT": cosT, "sinT": sinT})
        in_maps.append(m)
    return in_maps


def run(cfg, **inputs):
    nc = build_program(cfg)
    in_maps = make_in_maps(cfg, **inputs)
    res = run_bass_kernel_spmd(nc, in_maps, core_ids=list(range(cfg.NCORES)))
    c = cfg
    cps = c.NCORES // c.BATCH
    out = np.zeros((c.BATCH, c.SEQ, c.D), np.float32)
    for core in range(c.NCORES):
        b, r = core // cps, core % cps
        o = np.asarray(res.results[core]["out"]).reshape(c.NU * c.T, c.D)
        out[b, r * c.NU * c.T:(r + 1) * c.NU * c.T] = o
    return out


def kernel(x, w_in, attn_sinks, short_conv_w, w_out, ln1_g, ln1_b, ffn_w_up, ffn_conv_w, ffn_w_down, ln2_g, ln2_b):
    cfg = Cfg()
    return run(cfg, x=x, w_in=w_in, attn_sinks=attn_sinks, short_conv_w=short_conv_w, w_out=w_out, ln1_g=ln1_g,
               ln1_b=ln1_b, ffn_w_up=ffn_w_up, ffn_conv_w=ffn_conv_w, ffn_w_down=ffn_w_down, ln2_g=ln2_g, ln2_b=ln2_b)
```

```python
import math
import numpy as np
import concourse.bass as bass
import concourse.mybir as mybir
from concourse.bass_utils import run_bass_kernel_spmd

F32 = mybir.dt.float32
BF16 = mybir.dt.bfloat16
ALU = mybir.AluOpType
AF = mybir.ActivationFunctionType

ENGS = ("pe", "act", "dve", "pool", "sp")
NEG = -30000.0


class Cfg:
    def __init__(self, D=2048, NQ=16, F=5632, FQ=4, NB=8, NU=2, TILE=512, NW=256, BATCH=2, SEQ=8192,
                 NCORES=8, depth=1):
        self.D, self.NQ, self.F, self.FQ, self.NB, self.NU = D, NQ, F, FQ, NB, NU
        self.TILE, self.NW, self.BATCH, self.SEQ, self.NCORES = TILE, NW, BATCH, SEQ, NCORES
        self.KC = D // 128
        self.NQUAD = NQ // 4
        self.AW = NQ * 64
        self.ACH = self.AW // 128
        self.CW = D - self.AW
        self.CCH = self.CW // 128
        self.FCH = F // 128
        self.CQ = self.FCH // FQ
        self.T = NB * 128
        self.TX = self.T + 256
        self.TM = self.T + 128
        self.NJ = 5 + 2 * self.ACH + 3 * self.CCH
        self.ND = D // NW
        self.alpha = (2.0 * depth) ** 0.25
        assert self.FCH % FQ == 0 and self.ACH % 4 == 0 and D % NW == 0
        assert self.CQ * NW <= 2 * self.KC * 128


def tiles(a, b, step):
    out = []
    while a < b:
        n = min(step, b - a)
        out.append((a, n))
        a += n
    return out


class Prog:
    def __init__(self):
        self.q = {e: [] for e in ENGS}
        self.cnt = {e: 0 for e in ENGS}
        self.waited = {e: {} for e in ENGS}
        self.res = {}
        self.dcnt = {}
        self.semkeys = set("p_" + e for e in ENGS)
        self.sems = {}

    def _st(self, k):
        s = self.res.get(k)
        if s is None:
            s = self.res[k] = {"w": None, "r": {}}
        return s

    def _deps(self, eng, reads, writes):
        deps = {}

        def add(tok):
            if tok is not None and deps.get(tok[0], 0) < tok[1]:
                deps[tok[0]] = tok[1]

        for k in reads:
            add(self._st(k)["w"])
        for k in writes:
            s = self._st(k)
            add(s["w"])
            for sk, v in s["r"].items():
                add((sk, v))
        wt = self.waited[eng]
        for sk, v in deps.items():
            if wt.get(sk, 0) < v:
                wt[sk] = v
                self.q[eng].append(lambda e, sk=sk, v=v: e.wait_ge(self.sems[sk], v))

    def _mark(self, tok, reads, writes):
        for k in reads:
            s = self._st(k)
            if s["r"].get(tok[0], 0) < tok[1]:
                s["r"][tok[0]] = tok[1]
        for k in writes:
            s = self._st(k)
            s["w"] = tok
            s["r"] = {}

    def op(self, eng, fn, reads=(), writes=()):
        self._deps(eng, reads, writes)
        self.cnt[eng] += 1
        sk = "p_" + eng
        tok = (sk, self.cnt[eng])
        self.q[eng].append(lambda e, fn=fn, sk=sk: fn(e).then_inc(self.sems[sk], 1))
        self._mark(tok, reads, writes)
        return tok

    def dma(self, eng, semkey, out, in_, reads=(), writes=()):
        self._deps(eng, reads, writes)
        self.semkeys.add(semkey)
        self.dcnt[semkey] = self.dcnt.get(semkey, 0) + 1
        tok = (semkey, 16 * self.dcnt[semkey])
        self.q[eng].append(lambda e, out=out, in_=in_, semkey=semkey:
                           e.dma_start(out=out, in_=in_).then_inc(self.sems[semkey], 16))
        self._mark(tok, reads, writes)
        return tok

    def handoff(self, old_keys, new_keys):
        merged = {}
        for k in old_keys:
            s = self.res.get(k)
            if s is None:
                continue
            if s["w"] is not None and merged.get(s["w"][0], 0) < s["w"][1]:
                merged[s["w"][0]] = s["w"][1]
            for sk, v in s["r"].items():
                if merged.get(sk, 0) < v:
                    merged[sk] = v
        for k in new_keys:
            s = self._st(k)
            for sk, v in merged.items():
                if s["r"].get(sk, 0) < v:
                    s["r"][sk] = v

    def wait_all(self, eng, toks):
        wt = self.waited[eng]
        for sk, v in toks:
            if wt.get(sk, 0) < v:
                wt[sk] = v
                self.q[eng].append(lambda e, sk=sk, v=v: e.wait_ge(self.sems[sk], v))


def build_program(cfg):
    c = cfg
    D, KC, NQ, NQUAD, AW, ACH, CCH = c.D, c.KC, c.NQ, c.NQUAD, c.AW, c.ACH, c.CCH
    F, FCH, FQ, CQ, NB, NU, T, TX, TM = c.F, c.FCH, c.FQ, c.CQ, c.NB, c.NU, c.T, c.TX, c.TM
    TILE, NW, ND, NJ = c.TILE, c.NW, c.ND, c.NJ
    HS = KC * 128 * 2
    NHS = 8
    NPT = 6

    nc = bass.Bass("TRN2", target_bir_lowering=False)

    def din(name, shape):
        return nc.dram_tensor(name, list(shape), F32, kind="ExternalInput").ap()

    xT_d = din("xT", [NU, 128, KC, TX])
    xtm_d = din("xtm", [NU, TM, D])
    flag_d = din("flag", [NU, 128, 1])
    cos_d = din("cosT", [NU, 128, TX])
    sin_d = din("sinT", [NU, 128, TX])
    win_d = din("w_in", [NJ, 128, KC, 128])
    wout_d = din("w_out", [ND, 128, KC, NW])
    wup_d = din("w_up", [FCH, 2, 128, KC, 128])
    wdn_d = din("w_down", [FQ, ND, 128, CQ, NW])
    scw_d = din("scw", [128, CCH * 3])
    fcw_d = din("fcw", [128, FCH * 6])
    sink_d = din("sinks", [128, NQ])
    ln_d = din("ln", [4, 128, D])
    ident_d = din("ident", [128, 128])
    mask_d = din("masks", [2, 128, 512])
    out_d = nc.dram_tensor("out", [NU, T, D], F32, kind="ExternalOutput").ap()

    off = {}
    cur = 0

    def region(name, nbytes):
        nonlocal cur
        nbytes = (nbytes + 31) // 32 * 32
        off[name] = (cur, nbytes)
        cur += nbytes

    region("RX", KC * TX * 2)
    region("MIX", max(KC * TM * 2, 2 * D * 4 + D * 4 + D * 2, CQ * T * 2 + 2 * T * 4))
    region("RING", NHS * HS)
    region("CONST", 256 + 3 * 1024 + CCH * 12 + FCH * 24 + NQ * 4 + 2048)
    region("Z", (NB + 1) * D * 4)
    bc_need = (ACH * TM * 2 + 6 * TX * 2 + (NB + 2) * 2 * 66 * 2 + 32 + 2 * TX * 4 + NPT * 1024
               + 2 * AW * 2 + 4 * TILE * 4 + TILE * 4 + TX * 4 + TM * 4 + 128)
    g_need = 4 * ((2 + T) * 4 + 32)
    x_bytes = max(0, bc_need - (NB + 1) * D * 4, g_need)
    region("X", x_bytes)
    ARENA = cur
    assert ARENA <= 207 * 1024, ("SBUF over budget", ARENA)

    P = Prog()

    with nc.sbuf_tensor("arena", [128, ARENA // 4], F32) as arena, \
            nc.psum_tensor("ps", [128, 8, 512], F32) as ps:

        def view(base, dtype, shape):
            esz = 2 if dtype == BF16 else 4
            n = int(np.prod(shape)) * esz
            assert base % 4 == 0 and n % 4 == 0
            ap = arena[:, base // 4:(base + n) // 4]
            if dtype != F32:
                ap = ap.bitcast(dtype)
            if len(shape) == 2:
                ap = ap.rearrange("p (a b) -> p a b", a=shape[0])
            elif len(shape) == 3:
                ap = ap.rearrange("p (a b c) -> p a b c", a=shape[0], b=shape[1])
            return ap

        class Alloc:
            def __init__(self, base, limit):
                self.base, self.cur, self.limit = base, base, limit

            def get(self, dtype, shape):
                esz = 2 if dtype == BF16 else 4
                n = (int(np.prod(shape)) * esz + 31) // 32 * 32
                v = view(self.cur, dtype, shape)
                self.cur += n
                assert self.cur <= self.limit, ("region overflow", self.cur - self.limit)
                return v

        xT = view(off["RX"][0], BF16, [KC, TX])
        mixT = view(off["MIX"][0], BF16, [KC, TM])
        z = view(off["Z"][0], F32, [NB + 1, D])
        ca = Alloc(off["CONST"][0], off["CONST"][0] + off["CONST"][1])
        ident = ca.get(BF16, [128])
        maskD = ca.get(BF16, [512])
        maskP = ca.get(BF16, [512])
        maskF = ca.get(BF16, [512])
        scw = ca.get(F32, [CCH * 3])
        fcw = ca.get(F32, [FCH * 6])
        esink = ca.get(F32, [NQ])
        flag = ca.get(F32, [1])
        negb = ca.get(F32, [1])
        lnst2 = [ca.get(F32, [8, 6]) for _ in range(2)]
        lnmv2 = [ca.get(F32, [2]) for _ in range(2)]
        lnsd2 = [ca.get(F32, [1]) for _ in range(2)]
        lnrs2 = [ca.get(F32, [1]) for _ in range(2)]
        lnnm2 = [ca.get(F32, [1]) for _ in range(2)]
        den = ca.get(F32, [4])
        rden = ca.get(F32, [4])

        la = Alloc(off["MIX"][0], off["MIX"][0] + off["MIX"][1])
        ln_g = la.get(F32, [D])
        ln_b = la.get(F32, [D])
        ln_xb2 = [la.get(BF16, [D]) for _ in range(2)]
        fa = Alloc(off["MIX"][0], off["MIX"][0] + off["MIX"][1])
        hT = fa.get(BF16, [CQ, T])
        acc_a = fa.get(F32, [T])
        acc_g = fa.get(F32, [T])
        xa = Alloc(off["X"][0], off["X"][0] + off["X"][1])
        ubuf = [[xa.get(F32, [2 + T]) for _ in range(2)] for _ in range(2)]
        ba = Alloc(off["Z"][0], off["X"][0] + off["X"][1])
        qrot = ba.get(BF16, [ACH, TM])
        krot = ba.get(BF16, [2, TX])
        krotz = ba.get(BF16, [4, TX])
        vaug = ba.get(BF16, [NB + 2, 2, 66])
        cosT = ba.get(F32, [TX])
        sinT = ba.get(F32, [TX])
        pt = [ba.get(BF16, [512]) for _ in range(NPT)]
        attn_tm = [ba.get(BF16, [AW]) for _ in range(2)]
        rs = [ba.get(F32, [TILE]) for _ in range(4)]
        cs = ba.get(F32, [TILE])
        chbuf = ba.get(F32, [TX])
        cacc = ba.get(F32, [TM])

        def ringview(slot, shape):
            return view(off["RING"][0] + slot * HS, BF16, shape)

        K_BC = (["qrot%d" % i for i in range(ACH)] + ["krot%d" % i for i in range(2)] + ["krotz%d" % i for i in range(4)]
                + ["vaug", "tab", "cs", "chbuf", "cacc"]
                + ["pt%d" % i for i in range(NPT)] + ["attn0", "attn1"] + ["rs%d" % i for i in range(4)])
        K_Z = ["z%d" % i for i in range(NB + 1)]
        K_UB = ["ub%d%d" % (s, a) for s in range(2) for a in range(2)]
        K_MIX = ["mix%d" % i for i in range(KC)]
        K_LN = ["lnp", "lnxb0", "lnxb1"]
        K_FF = ["hT%d" % i for i in range(CQ)] + ["acca", "accg"]

        bank_ctr = [0]

        def nb():
            b = bank_ctr[0] % 8
            bank_ctr[0] += 1
            return b

        plan = []
        for u_ in range(NU):
            for j in range(5 + 2 * ACH + 3 * CCH):
                plan.append((("win", j), win_d[j], 1, [KC, 128]))
            for n_ in range(ND):
                plan.append((("wout", n_), wout_d[n_], 2, [KC, NW]))
            for q_ in range(FQ):
                for cq_ in range(CQ):
                    plan.append((("wup", q_ * CQ + cq_, 0), wup_d[q_ * CQ + cq_, 0], 1, [KC, 128]))
                    plan.append((("wup", q_ * CQ + cq_, 1), wup_d[q_ * CQ + cq_, 1], 1, [KC, 128]))
                for n_ in range(ND):
                    plan.append((("wdn", q_, n_), wdn_d[q_, n_], 2, [CQ, NW]))
        ring = {"ni": 0, "ng": 0, "cur": 0, "owner": [None] * NHS, "done": set(), "info": {}}

        def ring_try_issue():
            while ring["ni"] < len(plan):
                j = ring["ni"]
                _tag, src, nhs, shape = plan[j]
                cur_ = ring["cur"]
                if nhs == 2 and cur_ % 2 == 1:
                    cur_ += 1
                slot = cur_ % NHS
                owners = [ring["owner"][slot + i] for i in range(nhs)]
                if any(o is not None and o not in ring["done"] for o in owners):
                    break
                keys = ["ring%d" % (slot + i) for i in range(nhs)]
                v = ringview(slot, shape)
                P.dma("pool", "s_ring%d" % slot, v, src, reads=(), writes=keys)
                for i in range(nhs):
                    ring["owner"][slot + i] = j
                ring["info"][j] = (v, keys)
                ring["cur"] = cur_ + nhs
                ring["ni"] += 1

        def ring_get(tag):
            j = ring["ng"]
            if j >= ring["ni"]:
                ring_try_issue()
            assert j < ring["ni"], "ring stalled"
            assert plan[j][0] == tag, ("weight plan mismatch", plan[j][0], tag)
            ring["ng"] += 1
            v, keys = ring["info"][j]
            return v, keys, j

        def ring_release(*js):
            for j in js:
                ring["done"].add(j)
            ring_try_issue()

        def mm_group(outap, pairs, reads, writes):
            def fn(e, outap=outap, pairs=pairs):
                n = len(pairs)
                ins = None
                for i, (l, r) in enumerate(pairs):
                    ins = e.matmul(outap, l, r, start=(i == 0), stop=(i == n - 1))
                return ins
            return P.op("pe", fn, reads, writes)

        P.dma("sp", "s_c0", scw, scw_d, writes=["scw"])
        P.dma("sp", "s_c1", fcw, fcw_d, writes=["fcw"])
        P.dma("sp", "s_c2", esink, sink_d, writes=["esink"])
        P.dma("pool", "s_c3", ident, ident_d, writes=["ident"])
        P.dma("pool", "s_c4", maskD, mask_d[0], writes=["maskD"])
        P.dma("pool", "s_c5", maskP, mask_d[1], writes=["maskP"])
        P.op("act", lambda e: e.activation(out=esink, in_=esink, func=AF.Exp), reads=["esink"], writes=["esink"])

        out_toks = []

        ln_ctr = [0]

        def layer_norm(zrow, zkey, want_bf16):
            sl = ln_ctr[0] % 2
            ln_ctr[0] += 1
            lnst, lnmv, lnsd, lnrs, lnnm = lnst2[sl], lnmv2[sl], lnsd2[sl], lnrs2[sl], lnnm2[sl]
            kst, kmv, ksd, krs, knm = ("lnst%d" % sl, "lnmv%d" % sl, "lnsd%d" % sl, "lnrs%d" % sl, "lnnm%d" % sl)
            nchunk = (D + 511) // 512
            for j in range(nchunk):
                w = min(512, D - j * 512)
                P.op("dve", lambda e, j=j, w=w, lnst=lnst: e.bn_stats(out=lnst[:, j, :], in_=zrow[:, j * 512:j * 512 + w]),
                     reads=[zkey], writes=[kst])
            P.op("dve", lambda e, lnst=lnst, lnmv=lnmv: e.bn_aggr(out=lnmv, in_=lnst[:, 0:nchunk, :].rearrange("p a b -> p (a b)")),
                 reads=[kst], writes=[kmv])
            P.op("dve", lambda e, lnsd=lnsd, lnmv=lnmv: e.tensor_scalar(out=lnsd, in0=lnmv[:, 1:2], scalar1=1e-5, scalar2=None,
                                                                       op0=ALU.add), reads=[kmv], writes=[ksd])
            P.op("act", lambda e, lnsd=lnsd: e.activation(out=lnsd, in_=lnsd, func=AF.Sqrt), reads=[ksd], writes=[ksd])
            P.op("dve", lambda e, lnmv=lnmv: e.scalar_tensor_tensor(out=zrow, in0=zrow, scalar=lnmv[:, 0:1], in1=ln_g,
                                                                    op0=ALU.subtract, op1=ALU.mult),
                 reads=[zkey, kmv, "lnp"], writes=[zkey])
            P.op("dve", lambda e, lnrs=lnrs, lnsd=lnsd: e.reciprocal(out=lnrs, in_=lnsd), reads=[ksd], writes=[krs])
            P.op("dve", lambda e, lnrs=lnrs: e.scalar_tensor_tensor(out=zrow, in0=zrow, scalar=lnrs, in1=ln_b,
                                                                    op0=ALU.mult, op1=ALU.add),
                 reads=[zkey, krs, "lnp"], writes=[zkey])
            if want_bf16:
                xb = ln_xb2[sl]
                P.op("act", lambda e, xb=xb: e.activation(out=xb, in_=zrow, func=AF.Copy), reads=[zkey], writes=["lnxb%d" % sl])
            return sl

        def load_xT(u_):
            for k2 in range(0, KC, 2):
                P.dma("pool", "s_x%d" % k2, xT[:, k2:k2 + 2, :], xT_d[u_, :, k2:k2 + 2, :],
                      writes=["rx%d" % k2, "rx%d" % (k2 + 1)] + ["x1b%d" % bi for bi in range(NB + 1)])

        for u in range(NU):
            P.handoff(K_Z + K_UB, K_BC)
            P.handoff(K_LN + K_FF, K_MIX)
            P.dma("sp", "s_t0", cosT, cos_d[u], writes=["tab"])
            P.dma("sp", "s_t1", sinT, sin_d[u], writes=["tab"])
            P.dma("sp", "s_t2", flag, flag_d[u], writes=["flag"])
            P.op("dve", lambda e: e.tensor_scalar(out=negb, in0=flag, scalar1=-1.0, scalar2=-NEG,
                                                  op0=ALU.add, op1=ALU.mult), reads=["flag"], writes=["negb"])
            P.op("dve", lambda e: e.tensor_scalar(out=maskF, in0=maskP, scalar1=negb, scalar2=None, op0=ALU.add),
                 reads=["negb", "maskP"], writes=["maskF"])
            P.op("dve", lambda e: e.memset(vaug[:, :, :, 64:66], 1.0), writes=["vaug"])
            if u == 0:
                load_xT(0)
            RX = ["rx%d" % k for k in range(KC)]

            def proj_group(wv, wkeys, t0, n, b):
                mm_group(ps[:, b, 0:n], [(wv[:, kc, :], xT[:, kc, t0:t0 + n]) for kc in range(KC)],
                         reads=wkeys + RX, writes=["ps%d" % b])

            rope_ctr = [0]

            def rope_chunk(j0, dst, dkey, lo):
                wa, ka, ja_ = ring_get(("win", j0))
                wb, kb, jb_ = ring_get(("win", j0 + 1))
                for (t0, n) in tiles(lo, TX, TILE):
                    b1, b2 = nb(), nb()
                    proj_group(wa, ka, t0, n, b1)
                    proj_group(wb, kb, t0, n, b2)
                    si = 2 * (rope_ctr[0] % 2)
                    rope_ctr[0] += 1
                    r0, r1 = rs[si], rs[si + 1]
                    cc, ss = cosT[:, t0:t0 + n], sinT[:, t0:t0 + n]
                    P.op("dve", lambda e, b1=b1, cc=cc, n=n, r0=r0: e.tensor_tensor(out=r0[:, 0:n], in0=ps[:, b1, 0:n], in1=cc, op=ALU.mult),
                         reads=["ps%d" % b1, "tab"], writes=["rs%d" % si])
                    P.op("dve", lambda e, b2=b2, ss=ss, n=n, r1=r1: e.tensor_tensor(out=r1[:, 0:n], in0=ps[:, b2, 0:n], in1=ss, op=ALU.mult),
                         reads=["ps%d" % b2, "tab"], writes=["rs%d" % (si + 1)])
                    dd = dst[:, t0 - lo:t0 - lo + n]
                    P.op("pool", lambda e, dd=dd, n=n, r0=r0, r1=r1: e.tensor_tensor(out=dd, in0=r0[:, 0:n], in1=r1[:, 0:n], op=ALU.add),
                         reads=["rs%d" % si, "rs%d" % (si + 1)], writes=[dkey])
                ring_release(ja_, jb_)

            for kv in range(2):
                rope_chunk(2 * kv, krot[:, kv, :], "krot%d" % kv, 0)
                for half in range(2):
                    zi = 2 * kv + half
                    p0, z0 = 64 * half, 64 * (1 - half)
                    P.op("dve", lambda e, zi=zi, p0=p0, kv=kv: e.tensor_copy(out=krotz[p0:p0 + 64, zi, :], in_=krot[p0:p0 + 64, kv, :]),
                         reads=["krot%d" % kv], writes=["krotz%d" % zi])
                    P.op("dve", lambda e, zi=zi, z0=z0: e.memset(krotz[z0:z0 + 64, zi, :], 0.0), writes=["krotz%d" % zi])
            wv, kvk, jv_ = ring_get(("win", 4))
            for kb0 in range(0, NB + 2, 4):
                nk = min(4, NB + 2 - kb0)
                b = nb()

                def fnv(e, kb0=kb0, nk=nk, b=b, wv=wv):
                    ins = None
                    for j in range(nk):
                        kb = kb0 + j
                        for kc in range(KC):
                            ins = e.matmul(ps[:, b, j * 128:(j + 1) * 128], xT[:, kc, kb * 128:(kb + 1) * 128],
                                           wv[:, kc, :], start=(kc == 0), stop=(kc == KC - 1))
                    return ins
                P.op("pe", fnv, reads=kvk + RX, writes=["ps%d" % b])
                P.op("act", lambda e, kb0=kb0, nk=nk, b=b: e.activation(
                    out=vaug[:, kb0:kb0 + nk, :, 0:64],
                    in_=ps[:, b, 0:nk * 128].rearrange("p (a b c) -> p a b c", a=nk, b=2), func=AF.Copy),
                    reads=["ps%d" % b], writes=["vaug"])
            ring_release(jv_)
            for j in range(ACH):
                rope_chunk(5 + 2 * j, qrot[:, j, :], "qrot%d" % j, 128)
            for cc_ in range(CCH):
                j0 = 5 + 2 * ACH + 3 * cc_
                wC, kC, jC_ = ring_get(("win", j0))
                wH, kH, jH_ = ring_get(("win", j0 + 1))
                wB, kB_, jB_ = ring_get(("win", j0 + 2))
                for (t0, n) in tiles(126, TX, TILE):
                    b1, b2 = nb(), nb()
                    proj_group(wC, kC, t0, n, b1)
                    proj_group(wH, kH, t0, n, b2)
                    P.op("act", lambda e, b1=b1, n=n: e.activation(out=cs[:, 0:n], in_=ps[:, b1, 0:n], func=AF.Copy),
                         reads=["ps%d" % b1], writes=["cs"])
                    P.op("dve", lambda e, b2=b2, n=n, t0=t0: e.tensor_tensor(out=chbuf[:, t0:t0 + n], in0=ps[:, b2, 0:n],
                                                                      in1=cs[:, 0:n], op=ALU.mult),
                         reads=["ps%d" % b2, "cs"], writes=["chbuf"])
                ring_release(jC_, jH_)
                w0, w1, w2 = (scw[:, 3 * cc_ + k:3 * cc_ + k + 1] for k in range(3))
                P.op("dve", lambda e, w2=w2: e.tensor_scalar(out=cacc, in0=chbuf[:, 128:TX], scalar1=w2, scalar2=None, op0=ALU.mult),
                     reads=["chbuf", "scw"], writes=["cacc"])
                P.op("dve", lambda e, w1=w1: e.scalar_tensor_tensor(out=cacc, in0=chbuf[:, 127:TX - 1], scalar=w1, in1=cacc,
                                                                    op0=ALU.mult, op1=ALU.add),
                     reads=["chbuf", "scw", "cacc"], writes=["cacc"])
                P.op("dve", lambda e, w0=w0: e.scalar_tensor_tensor(out=cacc, in0=chbuf[:, 126:TX - 2], scalar=w0, in1=cacc,
                                                                    op0=ALU.mult, op1=ALU.add),
                     reads=["chbuf", "scw", "cacc"], writes=["cacc"])
                for (t0, n) in tiles(128, TX, TILE):
                    b = nb()
                    proj_group(wB, kB_, t0, n, b)
                    P.op("dve", lambda e, b=b, n=n, t0=t0, cc_=cc_: e.tensor_tensor(
                        out=mixT[:, ACH + cc_, t0 - 128:t0 - 128 + n], in0=ps[:, b, 0:n], in1=cacc[:, t0 - 128:t0 - 128 + n],
                        op=ALU.mult), reads=["ps%d" % b, "cacc"], writes=["mix%d" % (ACH + cc_)])
                ring_release(jB_)

            pt_ctr = [0]

            def emit_scores(i, m):
                qi = i - 1
                kv = (2 * m) // (ACH // 2)
                pts = []
                for half in range(2):
                    b = nb()
                    mkF = maskF if i == 2 else maskP
                    mkFk = "maskF" if i == 2 else "maskP"

                    def fns(e, b=b, mkF=mkF, kv=kv, m=m, qi=qi, half=half, i=i):
                        e.matmul(ps[:, b, 0:256], ident, mkF[:, 0:256], start=True, stop=False)
                        e.matmul(ps[:, b, 256:512], ident, maskD[:, 0:256], start=False, stop=False)
                        ins = None
                        p0 = 64 * half
                        for sel in range(2):
                            kb = i - 1 + sel
                            for cj in range(2):
                                col = (2 * sel + cj) * 128
                                ins = e.matmul(ps[:, b, col:col + 128],
                                               krotz[:, 2 * kv + half, kb * 128:(kb + 1) * 128],
                                               qrot[:, 2 * m + cj, qi * 128:(qi + 1) * 128],
                                               start=False, stop=(sel == 1 and cj == 1))
                        return ins
                    P.op("pe", fns, reads=["ident", mkFk, "maskD", "krotz%d" % (2 * kv + half), "qrot%d" % (2 * m), "qrot%d" % (2 * m + 1)],
                         writes=["ps%d" % b])
                    sl = pt_ctr[0] % NPT
                    pt_ctr[0] += 1
                    P.op("act", lambda e, b=b, sl=sl: e.activation(out=pt[sl], in_=ps[:, b, 0:512], func=AF.Exp, scale=0.125),
                         reads=["ps%d" % b], writes=["pt%d" % sl])
                    pts.append(sl)
                return pts

            def emit_pv(i, m, pts):
                qi = i - 1
                kv = (2 * m) // (ACH // 2)
                at = attn_tm[i % 2]
                atk = "attn%d" % (i % 2)
                bO = nb()

                def fno(e, bO=bO, pts=pts, kv=kv, i=i):
                    ins = None
                    for g in range(4):
                        pth = pt[pts[g % 2]]
                        cj = g // 2
                        e.matmul(ps[:, bO, g * 66:g * 66 + 66], pth[:, cj * 128:(cj + 1) * 128],
                                 vaug[:, i - 1, kv, 0:66], start=True, stop=False)
                        ins = e.matmul(ps[:, bO, g * 66:g * 66 + 66], pth[:, (2 + cj) * 128:(3 + cj) * 128],
                                       vaug[:, i, kv, 0:66], start=False, stop=True)
                    return ins
                P.op("pe", fno, reads=["pt%d" % pts[0], "pt%d" % pts[1], "vaug"], writes=["ps%d" % bO])
                pv = ps[:, bO, 0:264].rearrange("p (g e) -> p g e", e=66)
                P.op("dve", lambda e, pv=pv, m=m: e.tensor_tensor(out=den, in0=pv[:, :, 64], in1=esink[:, 4 * m:4 * m + 4], op=ALU.add),
                     reads=["ps%d" % bO, "esink"], writes=["den"])
                P.op("dve", lambda e: e.reciprocal(out=rden, in_=den), reads=["den"], writes=["rden"])
                for g in range(4):
                    h = 4 * m + g
                    P.op("dve", lambda e, pv=pv, g=g, h=h, at=at: e.tensor_scalar(
                        out=at[:, h * 64:(h + 1) * 64], in0=pv[:, g, 0:64], scalar1=rden[:, g:g + 1], scalar2=None,
                        op0=ALU.mult), reads=["ps%d" % bO, "rden"], writes=[atk])
                if m == NQUAD - 1:
                    for a0 in range(0, ACH, 8):
                        na = min(8, ACH - a0)
                        bT = nb()
                        pT = ps[:, bT, :].bitcast(BF16)

                        def fnt(e, a0=a0, na=na, pT=pT, at=at):
                            ins = None
                            for j in range(na):
                                ins = e.transpose(pT[:, j * 128:(j + 1) * 128], at[:, (a0 + j) * 128:(a0 + j + 1) * 128], ident)
                            return ins
                        P.op("pe", fnt, reads=[atk, "ident"], writes=["ps%d" % bT])
                        P.op("act", lambda e, a0=a0, na=na, pT=pT, qi=qi: e.activation(
                            out=mixT[:, a0:a0 + na, qi * 128:(qi + 1) * 128],
                            in_=pT[:, 0:na * 128].rearrange("p (a b) -> p a b", a=na), func=AF.Copy),
                            reads=["ps%d" % bT], writes=["mix%d" % j for j in range(a0, a0 + na)])

            prev = None
            for i in range(1, NB + 2):
                for m in range(NQUAD):
                    pts_ = emit_scores(i, m)
                    if prev is not None:
                        emit_pv(*prev)
                    prev = (i, m, pts_)
            emit_pv(*prev)

            P.handoff(K_BC, K_Z + K_UB)
            for i in range(NB + 1):
                P.dma("sp", "s_z%d" % i, z[:, i, :], xtm_d[u, i * 128:(i + 1) * 128, :], writes=["z%d" % i])
            for n_ in range(ND):
                wo, wok, jo_ = ring_get(("wout", n_))
                for i in range(NB + 1):
                    b = nb()
                    mm_group(ps[:, b, 0:NW], [(mixT[:, kc, i * 128:(i + 1) * 128], wo[:, kc, :]) for kc in range(KC)],
                             reads=wok + K_MIX, writes=["ps%d" % b])
                    zs = z[:, i, n_ * NW:(n_ + 1) * NW]
                    P.op("dve", lambda e, zs=zs, b=b: e.scalar_tensor_tensor(out=zs, in0=zs, scalar=c.alpha, in1=ps[:, b, 0:NW],
                                                                        op0=ALU.mult, op1=ALU.add),
                         reads=["ps%d" % b, "z%d" % i], writes=["z%d" % i])
                ring_release(jo_)

            P.handoff(K_MIX, K_LN)
            P.dma("sp", "s_l0", ln_g, ln_d[0], writes=["lnp"])
            P.dma("sp", "s_l1", ln_b, ln_d[1], writes=["lnp"])
            for i in range(NB + 1):
                lsl = layer_norm(z[:, i, :], "z%d" % i, True)
                ln_xb = ln_xb2[lsl]
                lxk = "lnxb%d" % lsl
                for k0 in range(0, KC, 8):
                    nk = min(8, KC - k0)
                    bT = nb()
                    pT = ps[:, bT, :].bitcast(BF16)

                    def fnt2(e, k0=k0, nk=nk, pT=pT, ln_xb=ln_xb):
                        ins = None
                        for j in range(nk):
                            ins = e.transpose(pT[:, j * 128:(j + 1) * 128], ln_xb[:, (k0 + j) * 128:(k0 + j + 1) * 128], ident)
                        return ins
                    P.op("pe", fnt2, reads=[lxk, "ident"], writes=["ps%d" % bT])
                    src = pT[:, 0:nk * 128].rearrange("p (a b) -> p a b", a=nk)
                    wk = ["rx%d" % k for k in range(k0, k0 + nk)] + ["x1b%d" % i]
                    if i == 0:
                        P.op("dve", lambda e, k0=k0, nk=nk, src=src: e.tensor_scalar(
                            out=xT[:, k0:k0 + nk, 0:2], in0=src[:, :, 126:128], scalar1=flag, scalar2=None, op0=ALU.mult),
                            reads=["ps%d" % bT, "flag"], writes=wk)
                    else:
                        P.op("act", lambda e, k0=k0, nk=nk, src=src, i=i: e.activation(
                            out=xT[:, k0:k0 + nk, 2 + (i - 1) * 128:2 + i * 128], in_=src, func=AF.Copy),
                            reads=["ps%d" % bT], writes=wk)

            P.handoff(K_LN, K_FF)
            ftiles = tiles(0, 2 + T, (2 + T + 2) // 3 if 2 + T > TILE else TILE)
            ftiles = tiles(0, 2 + T, min(TILE, -(-(2 + T) // (-(-(2 + T) // TILE)))))
            for q in range(FQ):
                for cq in range(CQ):
                    ch = q * CQ + cq
                    wa, wak, jwa_ = ring_get(("wup", ch, 0))
                    wg, wgk, jwg_ = ring_get(("wup", ch, 1))
                    us = ch % 2
                    ua, ug = ubuf[us]
                    for (t0, n) in ftiles:
                        b1, b2 = nb(), nb()
                        xk = ["x1b%d" % bi for bi in range(NB + 1)
                              if (max(0, 2 + (bi - 1) * 128) if bi > 0 else 0) < t0 + n and (2 + bi * 128 if bi > 0 else 2) > t0]
                        mm_group(ps[:, b1, 0:n], [(wa[:, kc, :], xT[:, kc, t0:t0 + n]) for kc in range(KC)],
                                 reads=wak + xk, writes=["ps%d" % b1])
                        mm_group(ps[:, b2, 0:n], [(wg[:, kc, :], xT[:, kc, t0:t0 + n]) for kc in range(KC)],
                                 reads=wgk + xk, writes=["ps%d" % b2])
                        P.op("act", lambda e, b1=b1, t0=t0, n=n, ua=ua: e.activation(out=ua[:, t0:t0 + n], in_=ps[:, b1, 0:n], func=AF.Copy),
                             reads=["ps%d" % b1], writes=["ub%d0" % us])
                        P.op("act", lambda e, b2=b2, t0=t0, n=n, ug=ug: e.activation(out=ug[:, t0:t0 + n], in_=ps[:, b2, 0:n], func=AF.Copy),
                             reads=["ps%d" % b2], writes=["ub%d1" % us])
                    ring_release(jwa_, jwg_)
                    for (ub_, ac, ag_, uk, ak) in ((ua, acc_a, 0, "ub%d0" % us, "acca"), (ug, acc_g, 1, "ub%d1" % us, "accg")):
                        wb_ = 6 * ch + 3 * ag_
                        w0, w1, w2 = (fcw[:, wb_ + k:wb_ + k + 1] for k in range(3))
                        P.op("dve", lambda e, ub_=ub_, ac=ac, w2=w2: e.tensor_scalar(out=ac, in0=ub_[:, 2:2 + T], scalar1=w2, scalar2=None, op0=ALU.mult),
                             reads=[uk, "fcw"], writes=[ak])
                        P.op("dve", lambda e, ub_=ub_, ac=ac, w1=w1: e.scalar_tensor_tensor(out=ac, in0=ub_[:, 1:1 + T], scalar=w1, in1=ac,
                                                                                   op0=ALU.mult, op1=ALU.add),
                             reads=[uk, "fcw", ak], writes=[ak])
                        P.op("dve", lambda e, ub_=ub_, ac=ac, w0=w0: e.scalar_tensor_tensor(out=ac, in0=ub_[:, 0:T], scalar=w0, in1=ac,
                                                                                   op0=ALU.mult, op1=ALU.add),
                             reads=[uk, "fcw", ak], writes=[ak])
                    P.op("act", lambda e: e.activation(out=acc_a, in_=acc_a, func=AF.Silu), reads=["acca"], writes=["acca"])
                    P.op("dve", lambda e, cq=cq: e.tensor_tensor(out=hT[:, cq, :], in0=acc_a, in1=acc_g, op=ALU.mult),
                         reads=["acca", "accg"], writes=["hT%d" % cq])
                for n_ in range(ND):
                    wd, wdk, jd_ = ring_get(("wdn", q, n_))
                    for i in range(NB):
                        b = nb()
                        mm_group(ps[:, b, 0:NW], [(hT[:, cq, i * 128:(i + 1) * 128], wd[:, cq, :]) for cq in range(CQ)],
                                 reads=wdk + ["hT%d" % cq for cq in range(CQ)], writes=["ps%d" % b])
                        zs = z[:, i + 1, n_ * NW:(n_ + 1) * NW]
                        zk = "z%d" % (i + 1)
                        if q == 0:
                            P.op("dve", lambda e, zs=zs, b=b: e.scalar_tensor_tensor(out=zs, in0=zs, scalar=c.alpha, in1=ps[:, b, 0:NW],
                                                                                op0=ALU.mult, op1=ALU.add),
                                 reads=["ps%d" % b, zk], writes=[zk])
                        else:
                            P.op("dve", lambda e, zs=zs, b=b: e.tensor_tensor(out=zs, in0=zs, in1=ps[:, b, 0:NW], op=ALU.add),
                                 reads=["ps%d" % b, zk], writes=[zk])
                    ring_release(jd_)

            if u + 1 < NU:
                load_xT(u + 1)
            P.handoff(K_FF, K_LN)
            P.dma("sp", "s_l0", ln_g, ln_d[2], writes=["lnp"])
            P.dma("sp", "s_l1", ln_b, ln_d[3], writes=["lnp"])
            for i in range(NB):
                layer_norm(z[:, i + 1, :], "z%d" % (i + 1), False)
                out_toks.append(P.dma("sp", "s_o%d" % i, out_d[u, i * 128:(i + 1) * 128, :], z[:, i + 1, :],
                                      reads=["z%d" % (i + 1)]))

        P.wait_all("sp", out_toks)

        import contextlib
        with contextlib.ExitStack() as es:
            for sk in sorted(P.semkeys):
                P.sems[sk] = es.enter_context(nc.semaphore(sk))
            with nc.Block() as block:
                @block.tensor
                def _(eng):
                    for f in P.q["pe"]:
                        f(eng)

                @block.scalar
                def _(eng):
                    for f in P.q["act"]:
                        f(eng)

                @block.vector
                def _(eng):
                    for f in P.q["dve"]:
                        f(eng)

                @block.gpsimd
                def _(eng):
                    for f in P.q["pool"]:
                        f(eng)

                @block.sync
                def _(eng):
                    for f in P.q["sp"]:
                        f(eng)
    return nc


def _win_cols(cfg):
    c = cfg
    AW, CW = c.AW, c.CW
    i = np.arange(128)
    sw = (i // 64) * 64 + ((i % 64) + 32) % 64
    cols = []
    for kv in range(2):
        cols.append(AW + kv * 64 + (i % 64))
        cols.append(AW + kv * 64 + (sw % 64))
    cols.append(AW + 128 + i)
    for j in range(c.ACH):
        cols.append(j * 128 + i)
        cols.append(j * 128 + sw)
    base = AW + 256
    for cc in range(c.CCH):
        cols.append(base + CW + cc * 128 + i)
        cols.append(base + 2 * CW + cc * 128 + i)
        cols.append(base + cc * 128 + i)
    return cols


def make_in_maps(cfg, x, w_in, attn_sinks, short_conv_w, w_out, ln1_g, ln1_b, ffn_w_up, ffn_conv_w, ffn_w_down,
                 ln2_g, ln2_b):
    c = cfg
    D, KC, T, TX, TM, NU = c.D, c.KC, c.T, c.TX, c.TM, c.NU
    f32 = np.float32
    x = np.asarray(x, f32)
    w_in = np.asarray(w_in, f32)[0]
    w_out = np.asarray(w_out, f32)[0]
    w_up = np.asarray(ffn_w_up, f32)[0]
    w_dn = np.asarray(ffn_w_down, f32)[0]
    cols = _win_cols(c)
    win_l = np.stack([w_in[:, cj].reshape(KC, 128, 128).transpose(1, 0, 2) for cj in cols]).astype(f32)
    wout_l = np.ascontiguousarray(w_out.reshape(KC, 128, c.ND, c.NW).transpose(2, 1, 0, 3))
    wup_l = np.ascontiguousarray(w_up.reshape(KC, 128, 2, c.FCH, 128).transpose(3, 2, 1, 0, 4))
    wdn_l = np.ascontiguousarray(w_dn.reshape(c.FQ, c.CQ, 128, c.ND, c.NW).transpose(0, 3, 2, 1, 4))
    scw = np.ascontiguousarray(np.asarray(short_conv_w, f32)[0].reshape(3, c.CCH, 128).transpose(2, 1, 0)).reshape(128, c.CCH * 3)
    fcw = np.ascontiguousarray(np.asarray(ffn_conv_w, f32)[0].reshape(3, 2, c.FCH, 128).transpose(3, 2, 1, 0)).reshape(128, c.FCH * 6)
    sinks = np.ascontiguousarray(np.broadcast_to(np.asarray(attn_sinks, f32)[0][None, :], (128, c.NQ)))
    ln = np.stack([np.broadcast_to(np.asarray(a, f32)[0][None, :], (128, D)) for a in (ln1_g, ln1_b, ln2_g, ln2_b)]).astype(f32)
    ident = np.eye(128, dtype=f32)
    kk = np.arange(128)[:, None]
    qq = np.arange(128)[None, :]
    mD = np.where(qq >= kk, 0.0, NEG).astype(f32)
    mP = np.where(kk > qq, 0.0, NEG).astype(f32)
    masks = np.stack([np.tile(mD, (1, 4)), np.tile(mP, (1, 4))]).astype(f32)
    inv_freq = (np.float32(10000.0) ** (-np.arange(32, dtype=f32) / np.float32(32))).astype(f32)
    shared = {"w_in": win_l, "w_out": wout_l, "w_up": wup_l, "w_down": wdn_l, "scw": scw, "fcw": fcw,
              "sinks": sinks, "ln": ln, "ident": ident, "masks": masks}
    cps = c.NCORES // c.BATCH
    per_core = c.SEQ // cps
    assert per_core == NU * T
    in_maps = []
    for core in range(c.NCORES):
        b, r = core // cps, core % cps
        xT = np.zeros((NU, 128, KC, TX), f32)
        xtm = np.zeros((NU, TM, D), f32)
        flag = np.zeros((NU, 128, 1), f32)
        cosT = np.zeros((NU, 128, TX), f32)
        sinT = np.zeros((NU, 128, TX), f32)
        for u in range(NU):
            s0 = r * per_core + u * T
            lo = s0 - 256
            seg = np.zeros((TX, D), f32)
            a = max(lo, 0)
            seg[a - lo:] = x[b, a:s0 + T]
            xT[u] = seg.T.reshape(KC, 128, TX).transpose(1, 0, 2)
            xtm[u] = seg[128:]
            flag[u] = 1.0 if s0 > 0 else 0.0
            pos = (np.arange(lo, s0 + T)).astype(f32)
            ang = pos[None, :] * inv_freq[:, None]
            cs_, sn_ = np.cos(ang).astype(f32), np.sin(ang).astype(f32)
            cosT[u] = np.tile(cs_, (4, 1))
            sinT[u] = np.concatenate([-sn_, sn_, -sn_, sn_], axis=0)
        m = dict(shared)
        m.update({"xT": xT, "xtm": xtm, "flag": flag, "cosT": cosT, "sinT": sinT})
        in_maps.append(m)
    return in_maps


def run(cfg, **inputs):
    nc = build_program(cfg)
    in_maps = make_in_maps(cfg, **inputs)
    res = run_bass_kernel_spmd(nc, in_maps, core_ids=list(range(cfg.NCORES)))
    c = cfg
    cps = c.NCORES // c.BATCH
    out = np.zeros((c.BATCH, c.SEQ, c.D), np.float32)
    for core in range(c.NCORES):
        b, r = core // cps, core % cps
        o = np.asarray(res.results[core]["out"]).reshape(c.NU * c.T, c.D)
        out[b, r * c.NU * c.T:(r + 1) * c.NU * c.T] = o
    return out


def kernel(x, w_in, attn_sinks, short_conv_w, w_out, ln1_g, ln1_b, ffn_w_up, ffn_conv_w, ffn_w_down, ln2_g, ln2_b):
    cfg = Cfg()
    return run(cfg, x=x, w_in=w_in, attn_sinks=attn_sinks, short_conv_w=short_conv_w, w_out=w_out, ln1_g=ln1_g,
               ln1_b=ln1_b, ffn_w_up=ffn_w_up, ffn_conv_w=ffn_conv_w, ffn_w_down=ffn_w_down, ln2_g=ln2_g, ln2_b=ln2_b)
```
